# Optimizing a Trainium2 kernel written in Bass

```python
import jax, jax.numpy as jnp
from jax import lax
import numpy as np

D_MODEL = 1024
BATCH = 16
SEQ = 2048
DEPTH = 2

GRID_W = 64
CTX_LEN = 256
HEAD_DIM = 64
ATTN_WIDTH = D_MODEL // 2
N_Q_HEADS = ATTN_WIDTH // HEAD_DIM
N_KV_HEADS = max(1, N_Q_HEADS // 4)
KV_WIDTH = N_KV_HEADS * HEAD_DIM
CONV_WIDTH = D_MODEL - ATTN_WIDTH
CONV_KERNEL = 31
FFN_HIDDEN = ((8 * D_MODEL // 3 + 127) // 128) * 128
FFN_KERNEL = 3
Q_BLOCK = 128
ROPE_THETA = 10000.0
EPS = 1e-6
IN_WIDTH = ATTN_WIDTH + 2 * KV_WIDTH + 2 * CONV_WIDTH
MOD_WIDTH = 6 * D_MODEL

kernel_name = "hybrid_attn_conformer_dit_block"


def rms_norm(x, g):
    x32 = x.astype(jnp.float32)
    y = x32 * lax.rsqrt(jnp.mean(x32 * x32, axis=-1, keepdims=True) + EPS)
    return y.astype(x.dtype) * g


def layer_norm(x, g, b):
    x32 = x.astype(jnp.float32)
    mu = jnp.mean(x32, axis=-1, keepdims=True)
    var = jnp.mean(jnp.square(x32 - mu), axis=-1, keepdims=True)
    y = (x32 - mu) * lax.rsqrt(var + EPS)
    return y.astype(x.dtype) * g + b


def depthwise_conv(x, w, b):
    k = w.shape[0]
    y = lax.conv_general_dilated(
        x, w[:, None, :].astype(x.dtype), window_strides=(1,),
        padding=[(k // 2, k // 2)], dimension_numbers=('NWC', 'WIO', 'NWC'),
        feature_group_count=x.shape[-1])
    return y + b


def axial_rope_tables(rows, dtype):
    row = jnp.repeat(jnp.arange(rows, dtype=jnp.int32), GRID_W)
    col = jnp.tile(jnp.arange(GRID_W, dtype=jnp.int32), rows)
    pos = jnp.stack([row, col], axis=-1).astype(jnp.float32)
    n_freq = HEAD_DIM // 4
    inv_freq = ROPE_THETA ** (-jnp.arange(n_freq, dtype=jnp.float32) / n_freq)
    ang = pos[:, :, None] * inv_freq
    return jnp.cos(ang).astype(dtype), jnp.sin(ang).astype(dtype)


def apply_rope(x, cos, sin):
    b, s, h, _ = x.shape
    xr = x.reshape(b, s, h, 2, 2, HEAD_DIM // 4)
    x1, x2 = xr[..., 0, :], xr[..., 1, :]
    cs, sn = cos[None, :, None], sin[None, :, None]
    out = jnp.stack([x1 * cs - x2 * sn, x2 * cs + x1 * sn], axis=-2)
    return out.reshape(b, s, h, HEAD_DIM)


def modulation(cvec, w_mod, b_mod):
    return jnp.split(jax.nn.silu(cvec) @ w_mod + b_mod, 6, axis=-1)


def modulate(x, g, shift, scale):
    return rms_norm(x, g) * (1 + scale) + shift


def q_heads(h, w_in, q_g):
    b, l, _ = h.shape
    q = (h @ w_in[:, :ATTN_WIDTH]).reshape(b, l, N_Q_HEADS, HEAD_DIM)
    return rms_norm(q, q_g)


def kv_heads(h, w_in, k_g):
    b, l, _ = h.shape
    kv = h @ w_in[:, ATTN_WIDTH:ATTN_WIDTH + 2 * KV_WIDTH]
    k, v = jnp.split(kv, 2, axis=-1)
    k = rms_norm(k.reshape(b, l, N_KV_HEADS, HEAD_DIM), k_g)
    return k, v.reshape(b, l, N_KV_HEADS, HEAD_DIM)


def conv_input(h, w_in):
    return h @ w_in[:, ATTN_WIDTH + 2 * KV_WIDTH:]


def conformer_conv(u, dw, dwb, ln_g, ln_b):
    a, gate = jnp.split(u, 2, axis=-1)
    y = depthwise_conv(a * jax.nn.sigmoid(gate), dw, dwb)
    return jax.nn.silu(layer_norm(y, ln_g, ln_b))


def latent_attention(q, k, v, kc, vc):
    b, s = q.shape[:2]
    nb = s // Q_BLOCK
    grp = N_Q_HEADS // N_KV_HEADS
    scale = HEAD_DIM ** -0.5
    qb = q.reshape(b, nb, Q_BLOCK, N_KV_HEADS, grp, HEAD_DIM).transpose(1, 0, 2, 3, 4, 5)

    def one_block(qi):
        s_lat = jnp.einsum('bqkgd,bskd->bkgqs', qi, k)
        s_ctx = jnp.einsum('bqkgd,bckd->bkgqc', qi, kc)
        sc = jnp.concatenate([s_lat, s_ctx], axis=-1).astype(jnp.float32) * scale
        p = jax.nn.softmax(sc, axis=-1).astype(v.dtype)
        return (jnp.einsum('bkgqs,bskd->bqkgd', p[..., :s], v)
                + jnp.einsum('bkgqc,bckd->bqkgd', p[..., s:], vc))

    o = lax.map(one_block, qb)
    return o.transpose(1, 0, 2, 3, 4, 5).reshape(b, s, ATTN_WIDTH)


def context_attention(q, k, v):
    b, l = q.shape[:2]
    grp = N_Q_HEADS // N_KV_HEADS
    qg = q.reshape(b, l, N_KV_HEADS, grp, HEAD_DIM)
    sc = jnp.einsum('bqkgd,bckd->bkgqc', qg, k).astype(jnp.float32) * (HEAD_DIM ** -0.5)
    p = jax.nn.softmax(sc, axis=-1).astype(v.dtype)
    return jnp.einsum('bkgqc,bckd->bqkgd', p, v).reshape(b, l, ATTN_WIDTH)


def conv_glu_ffn(h, w_up, dw, dwb, w_down):
    gate, val = jnp.split(h @ w_up, 2, axis=-1)
    return (jax.nn.silu(depthwise_conv(gate, dw, dwb)) * val) @ w_down


def setup_inputs(seed: int = 0) -> dict:
    key = jax.random.key(seed)
    ks = jax.random.split(key, 24)
    f32 = jnp.float32
    n = lambda k, shape, s: jax.random.normal(k, shape, f32) * s
    return {
        "x": n(ks[0], (BATCH, SEQ, D_MODEL), 1.0),
        "c": n(ks[1], (BATCH, D_MODEL), 1.0),
        "ctx": n(ks[2], (BATCH, CTX_LEN, D_MODEL), 1.0),
        "c_ctx": n(ks[3], (D_MODEL,), 1.0),
        "w_mod": n(ks[4], (DEPTH, D_MODEL, MOD_WIDTH), 0.5 * D_MODEL ** -0.5),
        "b_mod": n(ks[5], (DEPTH, MOD_WIDTH), 0.02),
        "norm1_g": 1.0 + n(ks[6], (DEPTH, D_MODEL), 0.05),
        "norm2_g": 1.0 + n(ks[7], (DEPTH, D_MODEL), 0.05),
        "w_in": n(ks[8], (DEPTH, D_MODEL, IN_WIDTH), D_MODEL ** -0.5),
        "q_norm_g": 1.0 + n(ks[9], (DEPTH, HEAD_DIM), 0.05),
        "k_norm_g": 1.0 + n(ks[10], (DEPTH, HEAD_DIM), 0.05),
        "conv_dw": n(ks[11], (DEPTH, CONV_KERNEL, CONV_WIDTH), CONV_KERNEL ** -0.5),
        "conv_dw_b": n(ks[12], (DEPTH, CONV_WIDTH), 0.02),
        "conv_ln_g": 1.0 + n(ks[13], (DEPTH, CONV_WIDTH), 0.05),
        "conv_ln_b": n(ks[14], (DEPTH, CONV_WIDTH), 0.02),
        "w_out": n(ks[15], (DEPTH, D_MODEL, D_MODEL), D_MODEL ** -0.5),
        "ffn_w_up": n(ks[16], (DEPTH, D_MODEL, 2 * FFN_HIDDEN), D_MODEL ** -0.5),
        "ffn_dw": n(ks[17], (DEPTH, FFN_KERNEL, FFN_HIDDEN), FFN_KERNEL ** -0.5),
        "ffn_dw_b": n(ks[18], (DEPTH, FFN_HIDDEN), 0.02),
        "ffn_w_down": n(ks[19], (DEPTH, FFN_HIDDEN, D_MODEL), FFN_HIDDEN ** -0.5),
        "final_g": 1.0 + n(ks[20], (D_MODEL,), 0.05),
    }


def reference(x, c, ctx, c_ctx, w_mod, b_mod, norm1_g, norm2_g, w_in, q_norm_g, k_norm_g,
              conv_dw, conv_dw_b, conv_ln_g, conv_ln_b, w_out, ffn_w_up, ffn_dw, ffn_dw_b,
              ffn_w_down, final_g):
    n_tok = x.shape[1]
    ROWS = n_tok // GRID_W
    cos, sin = axial_rope_tables(ROWS, x.dtype)
    cx = ctx
    for l in range(DEPTH):
        last = l == DEPTH - 1
        sh1, sc1, g1, sh2, sc2, g2 = [m[:, None, :] for m in modulation(c, w_mod[l], b_mod[l])]
        csh1, csc1, cg1, csh2, csc2, cg2 = modulation(c_ctx, w_mod[l], b_mod[l])

        h = modulate(x, norm1_g[l], sh1, sc1)
        hc = modulate(cx, norm1_g[l], csh1, csc1)
        q = apply_rope(q_heads(h, w_in[l], q_norm_g[l]), cos, sin)
        k, v = kv_heads(h, w_in[l], k_norm_g[l])
        k = apply_rope(k, cos, sin)
        kc, vc = kv_heads(hc, w_in[l], k_norm_g[l])
        attn = latent_attention(q, k, v, kc, vc)
        conv = conformer_conv(conv_input(h, w_in[l]), conv_dw[l], conv_dw_b[l],
                              conv_ln_g[l], conv_ln_b[l])
        x = x + g1 * (jnp.concatenate([attn, conv], axis=-1) @ w_out[l])

        h2 = modulate(x, norm2_g[l], sh2, sc2)
        x = x + g2 * conv_glu_ffn(h2, ffn_w_up[l], ffn_dw[l], ffn_dw_b[l], ffn_w_down[l])

        if not last:
            qc = q_heads(hc, w_in[l], q_norm_g[l])
            attn_c = context_attention(qc, kc, vc)
            conv_c = conformer_conv(conv_input(hc, w_in[l]), conv_dw[l], conv_dw_b[l],
                                    conv_ln_g[l], conv_ln_b[l])
            cx = cx + cg1 * (jnp.concatenate([attn_c, conv_c], axis=-1) @ w_out[l])
            hc2 = modulate(cx, norm2_g[l], csh2, csc2)
            cx = cx + cg2 * conv_glu_ffn(hc2, ffn_w_up[l], ffn_dw[l], ffn_dw_b[l], ffn_w_down[l])
    return rms_norm(x, final_g)
```

```python
import numpy as np
from contextlib import ExitStack
import concourse.bass as bass
import concourse.mybir as mybir
from concourse.bass_utils import run_bass_kernel_spmd

F32 = mybir.dt.float32
BF16 = mybir.dt.bfloat16
ALU = mybir.AluOpType
AF = mybir.ActivationFunctionType
_DS = {F32: 4, BF16: 2}

DEBUG_TAPS = None
EMBED_WAIT = True
STOP_AT = None
DBG_CORES = None


class _Stop(Exception):
    pass

COMPUTE = ("pe", "act", "dve", "pool")
BUCKET = 2048


def region(ap):
    t = ap.tensor
    rowlen = 1
    for s in list(t.shape)[1:]:
        rowlen *= s
    dsz = _DS[t.dtype]
    adsz = _DS[ap.dtype]
    rowlen_a = rowlen * dsz // adsz
    off = ap.offset
    apl = list(ap.ap)
    p0 = off // rowlen_a
    f0 = off % rowlen_a
    pstep, pcnt = apl[0]
    span = 0
    for st, cnt in apl[1:]:
        span += abs(st) * (cnt - 1)
    np_ = pcnt if pstep != 0 else 1
    return (t.name, p0, p0 + np_, f0 * adsz, (f0 + span + 1) * adsz)


class Instr:
    __slots__ = ("eng", "fn", "stream", "sidx", "waits", "signal", "clock", "inc")


class Prog:
    def __init__(self, nc):
        self.nc = nc
        self.per_eng = {e: [] for e in ("pe", "act", "dve", "pool", "sp")}
        self.stream_instrs = {}
        self.recs = {}
        self.seen = {e: {} for e in self.per_eng}
        self.n_instr = 0

    def _add(self, eng, fn, reads, writes, stream, inc):
        ins = Instr()
        ins.eng, ins.fn, ins.stream, ins.inc = eng, fn, stream, inc
        isdma = stream.startswith("dma:")
        ins.signal = isdma
        lst = self.stream_instrs.setdefault(stream, [])
        ins.sidx = len(lst)
        lst.append(ins)
        ins.waits = []
        seen = self.seen[eng]
        rregs = [region(a) for a in reads]
        wregs = [region(a) for a in writes]
        bank = lambda r: (r[0], 0, 128, (r[3] // 2048) * 2048, ((r[4] + 2047) // 2048) * 2048)
        wregs += [bank(r) for r in rregs if r[0].startswith("pd")]
        wregs = [(bank(r) if r[0].startswith("pd") else r) for r in wregs]
        rregs = [r for r in rregs if not r[0].startswith("pd")]
        deps = {}
        recs = self.recs
        for is_w, regs in ((False, rregs), (True, wregs)):
            for (name, p0, p1, b0, b1) in regs:
                for bk in range(b0 // BUCKET, (b1 - 1) // BUCKET + 1):
                    for r in recs.get((name, bk), ()):
                        if (is_w or r[6]) and r[0] < p1 and p0 < r[1] and r[2] < b1 and b0 < r[3]:
                            st = r[4]
                            if st == eng and eng == "pe":
                                continue
                            if r[5] > deps.get(st, -1):
                                deps[st] = r[5]
        for st, si in deps.items():
            if st.startswith("dma:"):
                si = len(self.stream_instrs[st]) - 1 if st != stream else ins.sidx - 1
                if si < 0:
                    continue
            if seen.get(st, -1) >= si:
                continue
            seen[st] = si
            tgt = self.stream_instrs[st][si]
            tgt.signal = True
            ins.waits.append((st, si))
            for ce, cv in zip(COMPUTE, tgt.clock):
                if cv > seen.get(ce, -1):
                    seen[ce] = cv
        ins.clock = tuple(seen.get(ce, -1) for ce in COMPUTE)
        for (name, p0, p1, b0, b1) in wregs:
            for bk in range(b0 // BUCKET, (b1 - 1) // BUCKET + 1):
                lo, hi = bk * BUCKET, (bk + 1) * BUCKET
                l = recs.setdefault((name, bk), [])
                l[:] = [r for r in l if not (p0 <= r[0] and r[1] <= p1 and b0 <= max(r[2], lo) and min(r[3], hi) <= b1)]
                l.append((p0, p1, b0, b1, stream, ins.sidx, True))
        for (name, p0, p1, b0, b1) in rregs:
            for bk in range(b0 // BUCKET, (b1 - 1) // BUCKET + 1):
                lo, hi = bk * BUCKET, (bk + 1) * BUCKET
                l = recs.setdefault((name, bk), [])
                l[:] = [r for r in l if not ((not r[6]) and r[4] == stream and p0 <= r[0] and r[1] <= p1
                                             and b0 <= max(r[2], lo) and min(r[3], hi) <= b1)]
                l.append((p0, p1, b0, b1, stream, ins.sidx, False))
        self.per_eng[eng].append(ins)
        self.n_instr += 1
        return ins

    def op(self, eng, fn, reads=(), writes=()):
        return self._add(eng, fn, reads, writes, eng, 1)

    def dma(self, queue, out, in_, group, **kw):
        reads = [in_] if in_.tensor.__class__.__name__ != "DRamTensorHandle" else []
        writes = [out] if out.tensor.__class__.__name__ != "DRamTensorHandle" else []
        return self._add(queue, lambda e: e.dma_start(out=out, in_=in_, **kw), reads, writes,
                         "dma:" + group, 16)

    def emit(self, final_wait_streams=()):
        nc = self.nc
        cnt = {}
        for st, lst in self.stream_instrs.items():
            c = 0
            arr = []
            for ins in lst:
                if ins.signal:
                    c += ins.inc
                arr.append(c)
            cnt[st] = arr
        with ExitStack() as es:
            sems = {st: es.enter_context(nc.semaphore("s_" + st.replace(":", "_")))
                    for st in self.stream_instrs}
            block = es.enter_context(nc.Block())
            handles = {"pe": block.tensor, "act": block.scalar, "dve": block.vector,
                       "pool": block.gpsimd, "sp": block.sync}

            def body_for(ename):
                def body(e):
                    for ins in self.per_eng[ename]:
                        nw = len(ins.waits) - (1 if EMBED_WAIT else 0)
                        for (st, si) in ins.waits[:max(nw, 0)]:
                            e.wait_ge(sems[st], cnt[st][si])
                        bi = ins.fn(e)
                        if EMBED_WAIT and ins.waits:
                            st, si = ins.waits[-1]
                            bi._wait_ge(sems[st], cnt[st][si])
                        if ins.signal:
                            bi.then_inc(sems[ins.stream], ins.inc)
                    if ename == "sp":
                        for st in final_wait_streams:
                            e.wait_ge(sems["dma:" + st], cnt["dma:" + st][-1])
                return body

            for ename, h in handles.items():
                if self.per_eng[ename] or ename == "sp":
                    h(body_for(ename))


D = 1024
S = 2048
LC = 256
T = S + LC
NB = 2
DEPTH = 2
HID = 2816
NJ = 22
EPS = 1e-6
WIN_COLS = 1920
GLU_W = 15 + S + 15 + 15 + LC + 15
GB_W = 1 + S + 1 + 1 + LC + 1

PL_N1G, PL_N2G, PL_QG, PL_KG = 0, 8, 16, 17
PL_CDW = 18
PL_CDWB = PL_CDW + 124
PL_LNG = PL_CDWB + 4
PL_LNB = PL_LNG + 4
PL_FDW = PL_LNB + 4
PL_FDWB = PL_FDW + 66
PL_BMOD = PL_FDWB + 22
PL_QROW = PL_BMOD + 144
PL_KROW = PL_QROW + 64
PL_W = PL_KROW + 64
C_ID, C_R, C_B64, C_O1024, C_O512 = 0, 128, 256, 384, 512

BLOCKS = [(0, 512), (512, 512), (1024, 512), (1536, 512), (2048, 256)]


def build_program():
    nc = bass.Bass("TRN2", target_bir_lowering=False)
    dt_in = lambda n, shp: nc.dram_tensor(n, shp, F32, kind="ExternalInput").ap()
    x_d = dt_in("x", [NB, S, D])
    ctx_d = dt_in("ctx", [NB, LC, D])
    cvec_d = dt_in("cvec", [128, 8 * 3])
    consts_d = dt_in("consts", [128, 5 * 128])
    rope_d = dt_in("rope", [2, 128, S])
    pl_d = dt_in("pl", [128, DEPTH * PL_W + 8])
    wmod_d = dt_in("wmod", [DEPTH, 6, 128, 8 * 1024])
    win_d = dt_in("win", [DEPTH, 128, 8 * WIN_COLS])
    wout_d = dt_in("wout", [DEPTH, 128, 8 * 1024])
    wup_d = dt_in("wup", [DEPTH, NJ, 128, 8 * 256])
    wdn_d = dt_in("wdn", [DEPTH, NJ, 128, 1024])
    out_d = nc.dram_tensor("out", [NB, S, D], F32, kind="ExternalOutput").ap()
    taps = {}
    if DEBUG_TAPS:
        for tn in DEBUG_TAPS:
            taps[tn] = nc.dram_tensor("tap_" + tn, [128, 8 * T], F32, kind="ExternalOutput").ap()

    ARENA_B = 124416
    with ExitStack() as es:
        sb = lambda n, shp, dt: es.enter_context(nc.sbuf_tensor(n, shp, dt))
        XT = sb("XT", [128, 8, T], F32)
        arena = sb("arena", [128, ARENA_B // 2], BF16)
        CF = sb("CF", [128, 5 * 128], F32)
        CB = sb("CB", [128, 5 * 128], BF16)
        PL = sb("PL", [128, DEPTH * PL_W + 8], F32)
        MOD = sb("MOD", [128, DEPTH * 144], F32)
        GS = sb("GS", [128, DEPTH * 2 * 24], F32)
        CV = sb("CV", [128, 24], F32)
        SCV = sb("SCV", [128, 24], BF16)
        MISC = sb("MISC", [128, 16], F32)
        PD = [es.enter_context(nc.psum_tensor("pd%d" % i, [128, 1024], F32)) for i in range(4)]
        PS = [PD[i // 2][:, (i % 2) * 512:(i % 2 + 1) * 512] for i in range(8)]
        P = Prog(nc)

        def view(off, shape, dt):
            n = 1
            for s_ in shape:
                n *= s_
            nbytes = n * _DS[dt]
            assert off % 4 == 0 and off + nbytes <= ARENA_B, (off, nbytes)
            a = arena[:, off // 2:(off + nbytes) // 2]
            if dt == F32:
                a = a.bitcast(F32)
            if len(shape) == 2:
                return a.rearrange("p (a b) -> p a b", a=shape[0])
            if len(shape) == 3:
                return a.rearrange("p (a b c) -> p a b c", a=shape[0], b=shape[1])
            return a

        def mm(out, lhsT, rhs, start, stop):
            P.op("pe", lambda e: e.matmul(out, lhsT, rhs, start=start, stop=stop), [lhsT, rhs], [out])

        def tr(out, in_, ident):
            P.op("pe", lambda e: e.transpose(out, in_, ident), [in_, ident], [out])

        def act(out, in_, func, bias=None, scale=1.0):
            reads = [in_]
            kw = {}
            if bias is not None:
                kw["bias"] = bias
                reads.append(bias)
            if not isinstance(scale, (int, float)):
                reads.append(scale)
            P.op("act", lambda e: e.activation(out, in_, func, scale=scale, **kw), reads, [out])

        def tt(out, a, b, op, eng="dve"):
            P.op(eng, lambda e: e.tensor_tensor(out, a, b, op), [a, b], [out])

        def ts(out, a, s1, s2, op0, op1=None, eng="dve"):
            reads = [a] + [s for s in (s1, s2) if s is not None and not isinstance(s, (int, float))]
            if op1 is None:
                P.op(eng, lambda e: e.tensor_scalar(out, a, s1, None, op0), reads, [out])
            else:
                P.op(eng, lambda e: e.tensor_scalar(out, a, s1, s2, op0, op1), reads, [out])

        def stt(out, in0, scalar, in1, op0, op1, eng="dve"):
            reads = [in0, in1] + ([scalar] if not isinstance(scalar, (int, float)) else [])
            P.op(eng, lambda e: e.scalar_tensor_tensor(out, in0, scalar, in1, op0, op1), reads, [out])

        def cp(out, in_, eng="dve"):
            if eng == "act":
                P.op("act", lambda e: e.copy(out, in_), [in_], [out])
            else:
                P.op(eng, lambda e: e.tensor_copy(out, in_), [in_], [out])

        def memset(out, val, eng="dve"):
            P.op(eng, lambda e: e.memset(out, val), [], [out])

        def recip(out, in_):
            P.op("dve", lambda e: e.reciprocal(out, in_), [in_], [out])

        def rsqrt_act(out, in_, tmp):
            act(tmp, in_, AF.Ln, bias=MISC[:, 0:1])
            act(out, tmp, AF.Exp, scale=-0.5)

        psi = [0]

        P.dma("sp", CF[:], consts_d, "cst")
        P.dma("sp", PL[:], pl_d, "cst")
        P.dma("sp", CV[:], cvec_d, "cst")
        cp(CB[:], CF[:])
        memset(MISC[:, 0:1], EPS)
        identF = CF[:, C_ID:C_ID + 128]
        identB = CB[:, C_ID:C_ID + 128]
        Rm = CB[:, C_R:C_R + 128]
        B64 = CB[:, C_B64:C_B64 + 128]
        O1024 = CB[:, C_O1024:C_O1024 + 128]
        O512 = CB[:, C_O512:C_O512 + 128]
        act(SCV[:], CV[:], AF.Silu)
        finalg = PL[:, DEPTH * PL_W:DEPTH * PL_W + 8]

        def plc(l, off, n=1):
            return PL[:, l * PL_W + off:l * PL_W + off + n]

        def modc(l, m, kc, j):
            o = l * 144 + (m * 8 + kc) * 3 + j
            return MOD[:, o:o + 1]

        def gsc(l, which, kc, j):
            o = (l * 2 + which) * 24 + kc * 3 + j
            return GS[:, o:o + 1]

        def stage(name):
            if STOP_AT == name:
                raise _Stop()

        wm_bufs = [view(0, [8, 1024], BF16), view(16384, [8, 1024], BF16)]
        psm = PS[7]
        wmi = 0
        for l in range(DEPTH):
            for m in range(6):
                wm = wm_bufs[wmi % 2]
                wmi += 1
                P.dma("pool", wm, wmod_d[l, m].rearrange("p (a b) -> p a b", a=8), "wm%d" % (wmi % 2))
                for oc in range(8):
                    o = (m * 8 + oc) * 3
                    for kc in range(8):
                        mm(psm[:, o:o + 3], wm[:, kc, oc * 128:(oc + 1) * 128], SCV[:, kc * 3:kc * 3 + 3],
                           kc == 0, kc == 7)
            tt(MOD[:, l * 144:(l + 1) * 144], psm[:, 0:144], plc(l, PL_BMOD, 144), ALU.add)
            for which, (m_sc, goff) in enumerate(((1, PL_N1G), (4, PL_N2G))):
                for j in range(3):
                    o = (l * 2 + which) * 24
                    src = MOD[:, l * 144 + m_sc * 24:l * 144 + (m_sc + 1) * 24].rearrange("p (k j) -> p k j", j=3)[:, :, j]
                    dst = GS[:, o:o + 24].rearrange("p (k j) -> p k j", j=3)[:, :, j]
                    ts(dst, src, 1.0, None, ALU.add)
                    tt(dst, dst, plc(l, goff, 8), ALU.mult)
            sc0 = MISC[:, 4:5]
            sc1 = MISC[:, 5:6]
            sq64 = MISC[:, 8:16]
            g2 = view(40000, [1, 128], F32)[:, 0, :]
            tt(g2, plc(l, PL_QROW, 128), plc(l, PL_QROW, 128), ALU.mult)
            P.op("dve", lambda e, g2=g2, sc0=sc0: e.tensor_reduce(sc0, g2[:, 0:64], mybir.AxisListType.X, ALU.max),
                 [g2[:, 0:64]], [sc0])
            P.op("dve", lambda e, g2=g2, sc1=sc1: e.tensor_reduce(sc1, g2[:, 64:128], mybir.AxisListType.X, ALU.max),
                 [g2[:, 64:128]], [sc1])
            tt(sc0, sc0, sc1, ALU.add)
            ts(MISC[:, 1 + l:2 + l], sc0, -4.0, None, ALU.mult)

        def rms_rstd(t0, n, sqb, rstd, lntmp, psb):
            for kc in range(8):
                sq = sqb[:, kc % 2, 0:n]
                act(sq, XT[:, kc, t0:t0 + n], AF.Square)
                mm(psb[:, 0:n], O1024, sq, kc == 0, kc == 7)
            rsqrt_act(rstd[:, 0:n], psb[:, 0:n], lntmp[:, 0:n])

        def modulated(t0, n, rstd, tmpA, l, which, j, dst_fn):
            m_sh = 0 if which == 0 else 3
            for kc in range(8):
                tmp = tmpA[:, kc % 2, 0:n]
                tt(tmp, XT[:, kc, t0:t0 + n], rstd[:, 0:n], ALU.mult)
                ts(dst_fn(kc), tmp, gsc(l, which, kc, j), modc(l, m_sh, kc, j), ALU.mult, ALU.add)

        def tap(name, src_ap_fn):
            if name in taps:
                P.dma("sp", taps[name].rearrange("p (a b) -> p a b", a=8), src_ap_fn(), "tap")

        try:
          stage('prologue')
          for s in range(NB):
            stg = [view(50176, [1, 1024], F32)[:, 0, :], view(54272, [1, 1024], F32)[:, 0, :]]
            for ti in range(18):
                st_ = stg[ti % 2]
                src = x_d[s, ti * 128:(ti + 1) * 128, :] if ti < 16 else ctx_d[s, (ti - 16) * 128:(ti - 15) * 128, :]
                P.dma("sp", st_, src, "stg%d" % (ti % 2))
                for half in range(2):
                    pb = PS[(ti * 2 + half) % 4]
                    for q in range(4):
                        kc = half * 4 + q
                        tr(pb[:, q * 128:(q + 1) * 128], st_[:, kc * 128:(kc + 1) * 128], identF)
                    dst = XT[:, half * 4:half * 4 + 4, ti * 128:(ti + 1) * 128]
                    srcp = pb[:, 0:512].rearrange("p (a b) -> p a b", a=4)
                    if half == 0:
                        cp(dst, srcp, "dve")
                    else:
                        cp(dst, srcp, "act")
            if s == 0:
                tap("x0", lambda: XT[:])
            stage('A%d' % s)

            for l in range(DEPTH):
                last = l == DEPTH - 1
                win = view(0, [8, WIN_COLS], BF16)
                diag = view(0, [4, 31, 128], BF16)
                wout = view(0, [8, 1024], BF16)
                ropeC = view(31744, [1, 512], F32)[:, 0, :]
                ropeS = view(35840, [1, 512], F32)[:, 0, :]
                hT = view(39936, [8, 512], BF16)
                convT = view(31744, [4, T], BF16)
                TB = 50176
                sqb = view(TB, [2, 512], BF16)
                sqq2 = view(TB + 2048, [2, 512], BF16)
                rstd = view(TB + 4096, [1, 512], F32)[:, 0, :]
                tmpA = view(TB + 6144, [2, 512], F32)
                qg2 = view(TB + 10240, [2, 512], BF16)
                rq = view(TB + 12288, [1, 512], F32)[:, 0, :]
                t1 = view(TB + 14336, [1, 512], F32)[:, 0, :]
                t2 = view(TB + 16384, [1, 512], F32)[:, 0, :]
                qT = view(68608, [4, T], BF16)
                kT = view(87040, [2, T], BF16)
                Vx = view(96256, [18, 2, 128], BF16)
                gluT = view(105472, [4, GLU_W], BF16)

                P.dma("pool", win, win_d[l].rearrange("p (a b) -> p a b", a=8), "win")
                for kv in range(2):
                    memset(Vx[:, :, kv, 64:128], 1.0)
                memset(gluT[:, :, 0:15], 0.0)
                memset(gluT[:, :, 15 + S:15 + S + 30], 0.0)
                memset(gluT[:, :, GLU_W - 15:GLU_W], 0.0)

                stage('Ba')
                for bi, (t0, n) in enumerate(BLOCKS):
                    isctx = bi == 4
                    j = 2 if isctx else s
                    rms_rstd(t0, n, sqb, rstd, t1, PS[0])
                    modulated(t0, n, rstd, tmpA, l, 0, j, lambda kc: hT[:, kc, 0:n])
                    stage('Bb')
                    if not isctx:
                        P.dma("sp", ropeC[:, 0:n], rope_d[0, :, t0:t0 + n], "rope")
                        P.dma("sp", ropeS[:, 0:n], rope_d[1, :, t0:t0 + n], "rope")
                    do_q = not (isctx and last)
                    do_conv = do_q
                    chunks = ([("q", i) for i in range(4)] if do_q else []) + [("k", 0), ("k", 1)]
                    def qk_stage1(idx, kind, ci):
                        oc = ci if kind == "q" else 4 + ci
                        pa = PS[1 + psi[0] % 3]
                        psi[0] += 1
                        for kc in range(8):
                            mm(pa[:, 0:n], win[:, kc, oc * 128:(oc + 1) * 128], hT[:, kc, 0:n], kc == 0, kc == 7)
                        gcol = plc(l, PL_QG if kind == "q" else PL_KG)
                        act(sqq2[:, idx % 2, 0:n], pa[:, 0:n], AF.Square)
                        ts(qg2[:, idx % 2, 0:n], pa[:, 0:n], gcol, None, ALU.mult)

                    def qk_stage2(idx, kind, ci):
                        sqq = sqq2[:, idx % 2, :]
                        qg = qg2[:, idx % 2, :]
                        mm(PS[4][:, 0:n], B64, sqq[:, 0:n], True, True)
                        if not isctx:
                            mm(PS[5][:, 0:n], Rm, qg[:, 0:n], True, True)
                        rsqrt_act(rq[:, 0:n], PS[4][:, 0:n], t2[:, 0:n])
                        dst = (qT if kind == "q" else kT)[:, ci, t0:t0 + n]
                        if not isctx:
                            tt(t1[:, 0:n], qg[:, 0:n], ropeC[:, 0:n], ALU.mult)
                            tt(t2[:, 0:n], PS[5][:, 0:n], ropeS[:, 0:n], ALU.mult)
                            tt(t1[:, 0:n], t1[:, 0:n], t2[:, 0:n], ALU.add)
                            tt(dst, t1[:, 0:n], rq[:, 0:n], ALU.mult)
                        else:
                            tt(dst, qg[:, 0:n], rq[:, 0:n], ALU.mult)

                    for idx, (kind, ci) in enumerate(chunks):
                        qk_stage1(idx, kind, ci)
                        if idx >= 1:
                            qk_stage2(idx - 1, *chunks[idx - 1])
                    qk_stage2(len(chunks) - 1, *chunks[-1])
                    stage('Bc')
                    if do_conv:
                        goff = 15 + t0 if not isctx else 15 + S + 30
                        for ci in range(4):
                            pa = PS[1 + (2 * ci) % 5]
                            pg = PS[1 + (2 * ci + 1) % 5]
                            for kc in range(8):
                                mm(pa[:, 0:n], win[:, kc, (6 + ci) * 128:(7 + ci) * 128], hT[:, kc, 0:n], kc == 0, kc == 7)
                            for kc in range(8):
                                mm(pg[:, 0:n], win[:, kc, (10 + ci) * 128:(11 + ci) * 128], hT[:, kc, 0:n], kc == 0, kc == 7)
                            act(t2[:, 0:n], pg[:, 0:n], AF.Sigmoid)
                            tt(gluT[:, ci, goff:goff + n], pa[:, 0:n], t2[:, 0:n], ALU.mult)
                    stage('Bd')
                    for tt_ in range(n // 128):
                        pv = PS[6 + tt_ % 2]
                        for kc in range(8):
                            mm(pv[:, 0:128], hT[:, kc, tt_ * 128:(tt_ + 1) * 128], win[:, kc, 1792:1920], kc == 0, kc == 7)
                        kt = (t0 + tt_ * 128) // 128
                        cp(Vx[:, kt, :, 0:64], pv[:, 0:128].rearrange("p (a b) -> p a b", a=2), "act")

                stage('B%d%d' % (s, l))
                for ci in range(4):
                    for k in range(31):
                        ts(diag[:, ci, k, :], identB, plc(l, PL_CDW + ci * 31 + k), None, ALU.mult)
                ysq = view(TB, [4, 512], BF16)
                mean2 = view(TB + 4096, [1, 512], F32)[:, 0, :]
                var = view(TB + 6144, [1, 512], F32)[:, 0, :]
                rstdc = view(TB + 8192, [1, 512], F32)[:, 0, :]
                tcv = view(TB + 10240, [2, 512], F32)
                dblocks = BLOCKS if not last else BLOCKS[:4]
                for bi, (t0, n) in enumerate(dblocks):
                    goff = t0 if bi < 4 else S + 30
                    for ci in range(4):
                        pc = PS[psi[0] % 2]
                        psi[0] += 1
                        for k in range(31):
                            mm(pc[:, 0:n], diag[:, ci, k, :], gluT[:, ci, goff + k:goff + k + n], k == 0, k == 30)
                        bcol = plc(l, PL_CDWB + ci)
                        act(convT[:, ci, t0:t0 + n], pc[:, 0:n], AF.Identity, bias=bcol)
                        act(ysq[:, ci, 0:n], pc[:, 0:n], AF.Square, bias=bcol)
                    for ci in range(4):
                        mm(PS[2][:, 0:n], O512, convT[:, ci, t0:t0 + n], ci == 0, ci == 3)
                    for ci in range(4):
                        mm(PS[3][:, 0:n], O512, ysq[:, ci, 0:n], ci == 0, ci == 3)
                    act(mean2[:, 0:n], PS[2][:, 0:n], AF.Square)
                    tt(var[:, 0:n], PS[3][:, 0:n], mean2[:, 0:n], ALU.subtract)
                    rsqrt_act(rstdc[:, 0:n], var[:, 0:n], mean2[:, 0:n])
                    for ci in range(4):
                        tc_ = tcv[:, ci % 2, 0:n]
                        tt(tc_, convT[:, ci, t0:t0 + n], PS[2][:, 0:n], ALU.subtract)
                        tt(tc_, tc_, rstdc[:, 0:n], ALU.mult)
                        act(convT[:, ci, t0:t0 + n], tc_, AF.Silu, bias=plc(l, PL_LNB + ci), scale=plc(l, PL_LNG + ci))

                stage('D%d%d' % (s, l))
                P.dma("pool", wout, wout_d[l].rearrange("p (a b) -> p a b", a=8), "wout")
                PT2 = view(TB, [3, 2, 512], BF16)
                rec = view(TB + 6144, [1, 512], F32)[:, 0, :]
                attnB = view(TB + 8192, [2, 4, 512], BF16)
                negB = MISC[:, 1 + l:2 + l]
                ablocks = BLOCKS if not last else BLOCKS[:4]
                pti = 0
                pending = []
                for bi, (t0, n) in enumerate(ablocks):
                    isctx = bi == 4
                    j = 2 if isctx else s
                    kts = list(range(18)) if not isctx else [16, 17]
                    pairs = [(kts[i], kts[i + 1]) for i in range(0, len(kts), 2)]
                    aT = attnB[:, bi % 2]
                    for h in range(8):
                        use_filler = not pending
                        hp = (h % 2) * 64
                        kv = h // 4
                        po = PS[4 + h % 2]
                        qv = qT[hp:hp + 64, h // 2, t0:t0 + n]

                        def issue_s(pi):
                            pd = PD[pi % 2]
                            for half, kt in enumerate(pairs[pi]):
                                mm(pd[:, half * 512:half * 512 + n], kT[hp:hp + 64, kv, kt * 128:(kt + 1) * 128], qv, True, True)
                            return pd

                        npair = len(pairs)
                        sq_ = [issue_s(0)]
                        if npair > 1:
                            sq_.append(issue_s(1))
                        for pi, (ka, kb) in enumerate(pairs):
                            pd = sq_[pi]
                            pt = PT2[:, pti % 3]
                            pti += 1
                            act(pt[:, :, 0:n], pd[:, :].rearrange("p (a b) -> p a b", a=2)[:, :, 0:n], AF.Exp,
                                bias=negB, scale=0.125)
                            if pi + 2 < npair:
                                sq_.append(issue_s(pi + 2))
                            for half, kt in enumerate((ka, kb)):
                                mm(po[:, 0:n], Vx[:, kt, kv, :], pt[:, half, 0:n], pi == 0 and half == 0,
                                   pi == npair - 1 and half == 1)
                            if use_filler:
                                mm(PS[7][:, 0:n], Vx[:, ka, kv, :], pt[:, 0, 0:n], True, True)
                            elif pi == min(3, npair - 1) and pending:
                                pending.pop(0)()
                        recip(rec[0:64, 0:n], po[64:128, 0:n])
                        tt(aT[hp:hp + 64, h // 2, 0:n], po[0:64, 0:n], rec[0:64, 0:n], ALU.mult)
                    def wout_item(oc, aT=aT, t0=t0, n=n, j=j):
                        pw = PS[6 + oc % 2]
                        for kc in range(8):
                            rhs = aT[:, kc, 0:n] if kc < 4 else convT[:, kc - 4, t0:t0 + n]
                            mm(pw[:, 0:n], wout[:, kc, oc * 128:(oc + 1) * 128], rhs, kc == 0, kc == 7)
                        stt(XT[:, oc, t0:t0 + n], pw[:, 0:n], modc(l, 2, oc, j), XT[:, oc, t0:t0 + n], ALU.mult, ALU.add)
                    while pending:
                        pending.pop(0)()
                    for oc in range(8):
                        pending.append(lambda oc=oc, f=wout_item: f(oc))
                while pending:
                    pending.pop(0)()
                if s == 0:
                    tap("mix%d" % l, lambda: XT[:])
                stage('C%d%d' % (s, l))

                h2T = view(0, [8, T], BF16)
                mbuf = view(36864, [8, T], BF16)
                wdn = view(73728, [8, 1024], BF16)
                wupb = [view(90112 + i * 4096, [8, 256], BF16) for i in range(3)]
                gbuf = view(102400, [1, GB_W], BF16)[:, 0, :]
                dg3 = view(107024, [2, 3, 128], BF16)
                FT = 108560
                sqb2 = view(FT, [4, 512], BF16)
                rstd2 = view(FT + 4096, [1, 512], F32)[:, 0, :]
                tmpA2 = view(FT + 6144, [2, 512], F32)
                asil = view(FT + 10240, [2, 512], F32)
                lnt = asil[:, 0, :]
                fblocks = BLOCKS if not last else BLOCKS[:4]
                for bi, (t0, n) in enumerate(fblocks):
                    j = 2 if bi == 4 else s
                    rms_rstd(t0, n, sqb2, rstd2, lnt, PS[0])
                    modulated(t0, n, rstd2, tmpA2, l, 1, j, lambda kc: h2T[:, kc, t0:t0 + n])
                memset(gbuf[:, 0:1], 0.0)
                memset(gbuf[:, 1 + S:1 + S + 2], 0.0)
                memset(gbuf[:, GB_W - 1:GB_W], 0.0)
                groups = [list(range(0, 8)), list(range(8, 15)), list(range(15, 22))]
                gi, vi = [0], [0]
                wi = 0
                for grp in groups:
                    for jj, jh in enumerate(grp):
                        wu = wupb[wi % 3]
                        P.dma("pool", wu, wup_d[l, jh].rearrange("p (a b) -> p a b", a=8), "wup%d" % (wi % 3))
                        P.dma("pool", wdn[:, jj, :], wdn_d[l, jh], "wdn")
                        dg = dg3[:, wi % 2]
                        wi += 1
                        for bi, (t0, n) in enumerate(fblocks):
                            pgt = PS[gi[0] % 3]
                            gi[0] += 1
                            for kc in range(8):
                                mm(pgt[:, 0:n], wu[:, kc, 0:128], h2T[:, kc, t0:t0 + n], kc == 0, kc == 7)
                            go = 1 + t0 if bi < 4 else 1 + S + 2
                            cp(gbuf[:, go:go + n], pgt[:, 0:n], "act")
                        for bi, (t0, n) in enumerate(fblocks):
                            go = t0 if bi < 4 else S + 2
                            cv = tmpA2[:, bi % 2, 0:n]
                            wk = [plc(l, PL_FDW + jh * 3 + k) for k in range(3)]
                            ts(cv, gbuf[:, go + 1:go + 1 + n], wk[1], plc(l, PL_FDWB + jh), ALU.mult, ALU.add)
                            stt(cv, gbuf[:, go:go + n], wk[0], cv, ALU.mult, ALU.add)
                            stt(cv, gbuf[:, go + 2:go + 2 + n], wk[2], cv, ALU.mult, ALU.add)
                            pvl = PS[3 + vi[0] % 3]
                            vi[0] += 1
                            for kc in range(8):
                                mm(pvl[:, 0:n], wu[:, kc, 128:256], h2T[:, kc, t0:t0 + n], kc == 0, kc == 7)
                            a_ = asil[:, bi % 2, 0:n]
                            act(a_, cv, AF.Silu)
                            tt(mbuf[:, jj, t0:t0 + n], a_, pvl[:, 0:n], ALU.mult)
                    for bi, (t0, n) in enumerate(fblocks):
                        j = 2 if bi == 4 else s
                        for oc in range(8):
                            pd = PS[6 + oc % 2]
                            for jj in range(len(grp)):
                                mm(pd[:, 0:n], wdn[:, jj, oc * 128:(oc + 1) * 128], mbuf[:, jj, t0:t0 + n],
                                   jj == 0, jj == len(grp) - 1)
                            stt(XT[:, oc, t0:t0 + n], pd[:, 0:n], modc(l, 5, oc, j), XT[:, oc, t0:t0 + n],
                                ALU.mult, ALU.add)
                if s == 0:
                    tap("ffn%d" % l, lambda: XT[:])
                stage('F%d%d' % (s, l))

            ofm = view(0, [8, 512], F32)
            ostg = [view(16384, [1, 1024], F32)[:, 0, :], view(20480, [1, 1024], F32)[:, 0, :]]
            sqb3 = view(24576, [4, 512], BF16)
            rstd3 = view(28672, [1, 512], F32)[:, 0, :]
            tmp3 = view(30720, [2, 512], F32)
            lnt3 = view(34816, [1, 512], F32)[:, 0, :]
            oi = 0
            for bi, (t0, n) in enumerate(BLOCKS[:4]):
                rms_rstd(t0, n, sqb3, rstd3, lnt3, PS[0])
                for kc in range(8):
                    tmp = tmp3[:, kc % 2, 0:n]
                    tt(tmp, XT[:, kc, t0:t0 + n], rstd3[:, 0:n], ALU.mult)
                    ts(ofm[:, kc, 0:n], tmp, finalg[:, kc:kc + 1], None, ALU.mult)
                for tq in range(4):
                    og = ostg[oi % 2]
                    for half in range(2):
                        pb = PS[1 + (oi * 2 + half) % 4]
                        for q in range(4):
                            kc = half * 4 + q
                            tr(pb[:, q * 128:(q + 1) * 128], ofm[:, kc, tq * 128:(tq + 1) * 128], identF)
                        cp(og[:, half * 512:(half + 1) * 512], pb[:, 0:512], "act" if half else "dve")
                    P.dma("sp", out_d[s, t0 + tq * 128:t0 + (tq + 1) * 128, :], og, "ost%d" % (oi % 2))
                    oi += 1

        except _Stop:
            pass
        fw = [g for g in ("ost0", "ost1", "tap") if ("dma:" + g) in P.stream_instrs]
        P.emit(final_wait_streams=fw)
        print("program: %d instrs, %d streams" % (P.n_instr, len(P.stream_instrs)),
              {k: len(v) for k, v in P.per_eng.items()})
    return nc


def _fm(v, nchunk):
    return np.ascontiguousarray(np.asarray(v, np.float32).reshape(nchunk, 128).T)


def _host_consts():
    c = np.zeros((128, 5 * 128), np.float32)
    c[:, C_ID:C_ID + 128] = np.eye(128, dtype=np.float32)
    R = np.zeros((128, 128), np.float32)
    for m in range(128):
        if (m % 32) < 16:
            R[m + 16, m] = -1.0
        else:
            R[m - 16, m] = 1.0
    c[:, C_R:C_R + 128] = R
    b = np.zeros((128, 128), np.float32)
    b[0:64, 0:64] = 1.0 / 64
    b[64:128, 64:128] = 1.0 / 64
    c[:, C_B64:C_B64 + 128] = b
    c[:, C_O1024:C_O1024 + 128] = 1.0 / 1024
    c[:, C_O512:C_O512 + 128] = 1.0 / 512
    t = np.arange(S)
    pos = np.stack([t // 64, t % 64], -1).astype(np.float32)
    inv_freq = (np.float32(10000.0) ** (-np.arange(16, dtype=np.float32) / np.float32(16))).astype(np.float32)
    ang = pos[:, :, None] * inv_freq
    cos, sin = np.cos(ang).astype(np.float32), np.sin(ang).astype(np.float32)
    rope = np.zeros((2, 128, S), np.float32)
    for p in range(128):
        d = p % 64
        rope[0, p] = cos[:, d // 32, d % 16]
        rope[1, p] = sin[:, d // 32, d % 16]
    return c, rope


def _prep_shared(inp):
    f = lambda k: np.asarray(inp[k], np.float32)
    consts, rope = _host_consts()
    pl = np.zeros((128, DEPTH * PL_W + 8), np.float32)
    for l in range(DEPTH):
        o = l * PL_W
        pl[:, o + PL_N1G:o + PL_N1G + 8] = _fm(f("norm1_g")[l], 8)
        pl[:, o + PL_N2G:o + PL_N2G + 8] = _fm(f("norm2_g")[l], 8)
        pl[:, o + PL_QG] = np.tile(f("q_norm_g")[l], 2)
        pl[:, o + PL_KG] = np.tile(f("k_norm_g")[l], 2)
        cdw = f("conv_dw")[l]
        pl[:, o + PL_CDW:o + PL_CDW + 124] = cdw.reshape(31, 4, 128).transpose(2, 1, 0).reshape(128, 124)
        pl[:, o + PL_CDWB:o + PL_CDWB + 4] = _fm(f("conv_dw_b")[l], 4)
        pl[:, o + PL_LNG:o + PL_LNG + 4] = _fm(f("conv_ln_g")[l], 4)
        pl[:, o + PL_LNB:o + PL_LNB + 4] = _fm(f("conv_ln_b")[l], 4)
        fdw = f("ffn_dw")[l]
        pl[:, o + PL_FDW:o + PL_FDW + 66] = fdw.reshape(3, NJ, 128).transpose(2, 1, 0).reshape(128, 66)
        pl[:, o + PL_FDWB:o + PL_FDWB + NJ] = _fm(f("ffn_dw_b")[l], NJ)
        bm = _fm(f("b_mod")[l], 48)
        pl[:, o + PL_BMOD:o + PL_BMOD + 144] = np.repeat(bm[:, :, None], 3, axis=2).reshape(128, 144)
        pl[:, o + PL_QROW:o + PL_QROW + 64] = f("q_norm_g")[l][None, :]
        pl[:, o + PL_KROW:o + PL_KROW + 64] = f("k_norm_g")[l][None, :]
    pl[:, DEPTH * PL_W:DEPTH * PL_W + 8] = _fm(f("final_g"), 8)

    def kmaj(w):
        n = w.shape[1]
        return np.ascontiguousarray(w.reshape(8, 128, n).transpose(1, 0, 2)).reshape(128, 8 * n)

    wmod = np.stack([np.stack([kmaj(f("w_mod")[l][:, m * 1024:(m + 1) * 1024]) for m in range(6)]) for l in range(DEPTH)])
    wins = []
    for l in range(DEPTH):
        w = f("w_in")[l]
        q = w[:, 0:512]
        k0, k1 = w[:, 512:576], w[:, 576:640]
        v = w[:, 640:768]
        a = w[:, 768:1280]
        g = w[:, 1280:1792]
        wr = np.concatenate([q, k0, k0, k1, k1, a, g, v], axis=1)
        assert wr.shape[1] == WIN_COLS
        wins.append(kmaj(wr))
    win = np.stack(wins)
    wout = np.stack([kmaj(f("w_out")[l]) for l in range(DEPTH)])
    wups = []
    for l in range(DEPTH):
        w = f("ffn_w_up")[l]
        per = []
        for jh in range(NJ):
            wj = np.concatenate([w[:, jh * 128:(jh + 1) * 128], w[:, HID + jh * 128:HID + (jh + 1) * 128]], axis=1)
            per.append(kmaj(wj))
        wups.append(np.stack(per))
    wup = np.stack(wups)
    wdn = np.stack([f("ffn_w_down")[l].reshape(NJ, 128, 1024) for l in range(DEPTH)])
    return dict(consts=consts, rope=rope, pl=pl, wmod=np.ascontiguousarray(wmod), win=win, wout=wout,
                wup=np.ascontiguousarray(wup), wdn=np.ascontiguousarray(wdn))


def kernel(**inputs):
    n_cores = 8
    run_cores = DBG_CORES or n_cores
    shared = _prep_shared(inputs)
    x = np.asarray(inputs["x"], np.float32)
    c = np.asarray(inputs["c"], np.float32)
    ctx = np.asarray(inputs["ctx"], np.float32)
    c_ctx = np.asarray(inputs["c_ctx"], np.float32)
    in_maps = []
    for i in range(run_cores):
        b0 = i * NB
        vecs = np.stack([c[b0], c[b0 + 1], c_ctx], axis=-1)
        cvec = np.ascontiguousarray(vecs.reshape(8, 128, 3).transpose(1, 0, 2)).reshape(128, 24)
        m = dict(shared)
        m["x"] = np.ascontiguousarray(x[b0:b0 + NB])
        m["ctx"] = np.ascontiguousarray(ctx[b0:b0 + NB])
        m["cvec"] = cvec
        in_maps.append(m)
    nc = build_program()
    res = run_bass_kernel_spmd(nc, in_maps, core_ids=list(range(run_cores)))
    kernel.last_results = res
    out = np.concatenate([np.asarray(r["out"]) for r in res.results], axis=0)
    return out.astype(np.float32)
```

```python
import numpy as np
from contextlib import ExitStack
import concourse.bass as bass
import concourse.mybir as mybir
from concourse.bass_utils import run_bass_kernel_spmd

F32 = mybir.dt.float32
BF16 = mybir.dt.bfloat16
ALU = mybir.AluOpType
AF = mybir.ActivationFunctionType
_DS = {F32: 4, BF16: 2}

DEBUG_TAPS = None
EMBED_WAIT = True
STOP_AT = None
DBG_CORES = None


class _Stop(Exception):
    pass

COMPUTE = ("pe", "act", "dve", "pool")
BUCKET = 2048


def region(ap):
    t = ap.tensor
    rowlen = 1
    for s in list(t.shape)[1:]:
        rowlen *= s
    dsz = _DS[t.dtype]
    adsz = _DS[ap.dtype]
    rowlen_a = rowlen * dsz // adsz
    off = ap.offset
    apl = list(ap.ap)
    p0 = off // rowlen_a
    f0 = off % rowlen_a
    pstep, pcnt = apl[0]
    span = 0
    for st, cnt in apl[1:]:
        span += abs(st) * (cnt - 1)
    np_ = pcnt if pstep != 0 else 1
    return (t.name, p0, p0 + np_, f0 * adsz, (f0 + span + 1) * adsz)


class Instr:
    __slots__ = ("eng", "fn", "stream", "sidx", "waits", "signal", "clock", "inc")


class Prog:
    def __init__(self, nc):
        self.nc = nc
        self.per_eng = {e: [] for e in ("pe", "act", "dve", "pool", "sp")}
        self.stream_instrs = {}
        self.recs = {}
        self.seen = {e: {} for e in self.per_eng}
        self.n_instr = 0

    def _add(self, eng, fn, reads, writes, stream, inc):
        ins = Instr()
        ins.eng, ins.fn, ins.stream, ins.inc = eng, fn, stream, inc
        isdma = stream.startswith("dma:")
        ins.signal = isdma
        lst = self.stream_instrs.setdefault(stream, [])
        ins.sidx = len(lst)
        lst.append(ins)
        ins.waits = []
        seen = self.seen[eng]
        rregs = [region(a) for a in reads]
        wregs = [region(a) for a in writes]
        bank = lambda r: (r[0], 0, 128, (r[3] // 2048) * 2048, ((r[4] + 2047) // 2048) * 2048)
        wregs += [bank(r) for r in rregs if r[0].startswith("pd")]
        wregs = [(bank(r) if r[0].startswith("pd") else r) for r in wregs]
        rregs = [r for r in rregs if not r[0].startswith("pd")]
        deps = {}
        recs = self.recs
        for is_w, regs in ((False, rregs), (True, wregs)):
            for (name, p0, p1, b0, b1) in regs:
                for bk in range(b0 // BUCKET, (b1 - 1) // BUCKET + 1):
                    for r in recs.get((name, bk), ()):
                        if (is_w or r[6]) and r[0] < p1 and p0 < r[1] and r[2] < b1 and b0 < r[3]:
                            st = r[4]
                            if st == eng and eng == "pe":
                                continue
                            if r[5] > deps.get(st, -1):
                                deps[st] = r[5]
        for st, si in deps.items():
            if st.startswith("dma:"):
                si = len(self.stream_instrs[st]) - 1 if st != stream else ins.sidx - 1
                if si < 0:
                    continue
            if seen.get(st, -1) >= si:
                continue
            seen[st] = si
            tgt = self.stream_instrs[st][si]
            tgt.signal = True
            ins.waits.append((st, si))
            for ce, cv in zip(COMPUTE, tgt.clock):
                if cv > seen.get(ce, -1):
                    seen[ce] = cv
        ins.clock = tuple(seen.get(ce, -1) for ce in COMPUTE)
        for (name, p0, p1, b0, b1) in wregs:
            for bk in range(b0 // BUCKET, (b1 - 1) // BUCKET + 1):
                lo, hi = bk * BUCKET, (bk + 1) * BUCKET
                l = recs.setdefault((name, bk), [])
                l[:] = [r for r in l if not (p0 <= r[0] and r[1] <= p1 and b0 <= max(r[2], lo) and min(r[3], hi) <= b1)]
                l.append((p0, p1, b0, b1, stream, ins.sidx, True))
        for (name, p0, p1, b0, b1) in rregs:
            for bk in range(b0 // BUCKET, (b1 - 1) // BUCKET + 1):
                lo, hi = bk * BUCKET, (bk + 1) * BUCKET
                l = recs.setdefault((name, bk), [])
                l[:] = [r for r in l if not ((not r[6]) and r[4] == stream and p0 <= r[0] and r[1] <= p1
                                             and b0 <= max(r[2], lo) and min(r[3], hi) <= b1)]
                l.append((p0, p1, b0, b1, stream, ins.sidx, False))
        self.per_eng[eng].append(ins)
        self.n_instr += 1
        return ins

    def op(self, eng, fn, reads=(), writes=()):
        return self._add(eng, fn, reads, writes, eng, 1)

    def dma(self, queue, out, in_, group, **kw):
        reads = [in_] if in_.tensor.__class__.__name__ != "DRamTensorHandle" else []
        writes = [out] if out.tensor.__class__.__name__ != "DRamTensorHandle" else []
        return self._add(queue, lambda e: e.dma_start(out=out, in_=in_, **kw), reads, writes,
                         "dma:" + group, 16)

    def emit(self, final_wait_streams=()):
        nc = self.nc
        cnt = {}
        for st, lst in self.stream_instrs.items():
            c = 0
            arr = []
            for ins in lst:
                if ins.signal:
                    c += ins.inc
                arr.append(c)
            cnt[st] = arr
        with ExitStack() as es:
            sems = {st: es.enter_context(nc.semaphore("s_" + st.replace(":", "_")))
                    for st in self.stream_instrs}
            block = es.enter_context(nc.Block())
            handles = {"pe": block.tensor, "act": block.scalar, "dve": block.vector,
                       "pool": block.gpsimd, "sp": block.sync}

            def body_for(ename):
                def body(e):
                    for ins in self.per_eng[ename]:
                        nw = len(ins.waits) - (1 if EMBED_WAIT else 0)
                        for (st, si) in ins.waits[:max(nw, 0)]:
                            e.wait_ge(sems[st], cnt[st][si])
                        bi = ins.fn(e)
                        if EMBED_WAIT and ins.waits:
                            st, si = ins.waits[-1]
                            bi._wait_ge(sems[st], cnt[st][si])
                        if ins.signal:
                            bi.then_inc(sems[ins.stream], ins.inc)
                    if ename == "sp":
                        for st in final_wait_streams:
                            e.wait_ge(sems["dma:" + st], cnt["dma:" + st][-1])
                return body

            for ename, h in handles.items():
                if self.per_eng[ename] or ename == "sp":
                    h(body_for(ename))


D = 1024
S = 2048
LC = 256
T = S + LC
NB = 2
DEPTH = 2
HID = 2816
NJ = 22
EPS = 1e-6
WIN_COLS = 1920
GLU_W = 15 + S + 15 + 15 + LC + 15
GB_W = 1 + S + 1 + 1 + LC + 1

PL_N1G, PL_N2G, PL_QG, PL_KG = 0, 8, 16, 17
PL_CDW = 18
PL_CDWB = PL_CDW + 124
PL_LNG = PL_CDWB + 4
PL_LNB = PL_LNG + 4
PL_FDW = PL_LNB + 4
PL_FDWB = PL_FDW + 66
PL_BMOD = PL_FDWB + 22
PL_QROW = PL_BMOD + 144
PL_KROW = PL_QROW + 64
PL_W = PL_KROW + 64
C_ID, C_R, C_B64, C_O1024, C_O512 = 0, 128, 256, 384, 512

BLOCKS = [(0, 512), (512, 512), (1024, 512), (1536, 512), (2048, 256)]


def build_program():
    nc = bass.Bass("TRN2", target_bir_lowering=False)
    dt_in = lambda n, shp: nc.dram_tensor(n, shp, F32, kind="ExternalInput").ap()
    x_d = dt_in("x", [NB, S, D])
    ctx_d = dt_in("ctx", [NB, LC, D])
    cvec_d = dt_in("cvec", [128, 8 * 3])
    consts_d = dt_in("consts", [128, 5 * 128])
    rope_d = dt_in("rope", [2, 128, S])
    pl_d = dt_in("pl", [128, DEPTH * PL_W + 8])
    wmod_d = dt_in("wmod", [DEPTH, 6, 128, 8 * 1024])
    win_d = dt_in("win", [DEPTH, 128, 8 * WIN_COLS])
    wout_d = dt_in("wout", [DEPTH, 128, 8 * 1024])
    wup_d = dt_in("wup", [DEPTH, NJ, 128, 8 * 256])
    wdn_d = dt_in("wdn", [DEPTH, NJ, 128, 1024])
    out_d = nc.dram_tensor("out", [NB, S, D], F32, kind="ExternalOutput").ap()
    taps = {}
    if DEBUG_TAPS:
        for tn in DEBUG_TAPS:
            taps[tn] = nc.dram_tensor("tap_" + tn, [128, 8 * T], F32, kind="ExternalOutput").ap()

    ARENA_B = 124416
    with ExitStack() as es:
        sb = lambda n, shp, dt: es.enter_context(nc.sbuf_tensor(n, shp, dt))
        XT = sb("XT", [128, 8, T], F32)
        arena = sb("arena", [128, ARENA_B // 2], BF16)
        CF = sb("CF", [128, 5 * 128], F32)
        CB = sb("CB", [128, 5 * 128], BF16)
        PL = sb("PL", [128, DEPTH * PL_W + 8], F32)
        MOD = sb("MOD", [128, DEPTH * 144], F32)
        GS = sb("GS", [128, DEPTH * 2 * 24], F32)
        CV = sb("CV", [128, 24], F32)
        SCV = sb("SCV", [128, 24], BF16)
        MISC = sb("MISC", [128, 16], F32)
        PD = [es.enter_context(nc.psum_tensor("pd%d" % i, [128, 1024], F32)) for i in range(4)]
        PS = [PD[i // 2][:, (i % 2) * 512:(i % 2 + 1) * 512] for i in range(8)]
        P = Prog(nc)

        def view(off, shape, dt):
            n = 1
            for s_ in shape:
                n *= s_
            nbytes = n * _DS[dt]
            assert off % 4 == 0 and off + nbytes <= ARENA_B, (off, nbytes)
            a = arena[:, off // 2:(off + nbytes) // 2]
            if dt == F32:
                a = a.bitcast(F32)
            if len(shape) == 2:
                return a.rearrange("p (a b) -> p a b", a=shape[0])
            if len(shape) == 3:
                return a.rearrange("p (a b c) -> p a b c", a=shape[0], b=shape[1])
            return a

        def mm(out, lhsT, rhs, start, stop):
            P.op("pe", lambda e: e.matmul(out, lhsT, rhs, start=start, stop=stop), [lhsT, rhs], [out])

        def tr(out, in_, ident):
            P.op("pe", lambda e: e.transpose(out, in_, ident), [in_, ident], [out])

        def act(out, in_, func, bias=None, scale=1.0):
            reads = [in_]
            kw = {}
            if bias is not None:
                kw["bias"] = bias
                reads.append(bias)
            if not isinstance(scale, (int, float)):
                reads.append(scale)
            P.op("act", lambda e: e.activation(out, in_, func, scale=scale, **kw), reads, [out])

        def tt(out, a, b, op, eng="dve"):
            P.op(eng, lambda e: e.tensor_tensor(out, a, b, op), [a, b], [out])

        def ts(out, a, s1, s2, op0, op1=None, eng="dve"):
            reads = [a] + [s for s in (s1, s2) if s is not None and not isinstance(s, (int, float))]
            if op1 is None:
                P.op(eng, lambda e: e.tensor_scalar(out, a, s1, None, op0), reads, [out])
            else:
                P.op(eng, lambda e: e.tensor_scalar(out, a, s1, s2, op0, op1), reads, [out])

        def stt(out, in0, scalar, in1, op0, op1, eng="dve"):
            reads = [in0, in1] + ([scalar] if not isinstance(scalar, (int, float)) else [])
            P.op(eng, lambda e: e.scalar_tensor_tensor(out, in0, scalar, in1, op0, op1), reads, [out])

        def cp(out, in_, eng="dve"):
            if eng == "act":
                P.op("act", lambda e: e.copy(out, in_), [in_], [out])
            else:
                P.op(eng, lambda e: e.tensor_copy(out, in_), [in_], [out])

        def memset(out, val, eng="dve"):
            P.op(eng, lambda e: e.memset(out, val), [], [out])

        def recip(out, in_):
            P.op("dve", lambda e: e.reciprocal(out, in_), [in_], [out])

        def rsqrt_act(out, in_, tmp):
            act(tmp, in_, AF.Ln, bias=MISC[:, 0:1])
            act(out, tmp, AF.Exp, scale=-0.5)

        psi = [0]

        P.dma("sp", CF[:], consts_d, "cst")
        P.dma("sp", PL[:], pl_d, "cst")
        P.dma("sp", CV[:], cvec_d, "cst")
        cp(CB[:], CF[:])
        memset(MISC[:, 0:1], EPS)
        identF = CF[:, C_ID:C_ID + 128]
        identB = CB[:, C_ID:C_ID + 128]
        Rm = CB[:, C_R:C_R + 128]
        B64 = CB[:, C_B64:C_B64 + 128]
        O1024 = CB[:, C_O1024:C_O1024 + 128]
        O512 = CB[:, C_O512:C_O512 + 128]
        act(SCV[:], CV[:], AF.Silu)
        finalg = PL[:, DEPTH * PL_W:DEPTH * PL_W + 8]

        def plc(l, off, n=1):
            return PL[:, l * PL_W + off:l * PL_W + off + n]

        def modc(l, m, kc, j):
            o = l * 144 + (m * 8 + kc) * 3 + j
            return MOD[:, o:o + 1]

        def gsc(l, which, kc, j):
            o = (l * 2 + which) * 24 + kc * 3 + j
            return GS[:, o:o + 1]

        def stage(name):
            if STOP_AT == name:
                raise _Stop()

        wm_bufs = [view(0, [8, 1024], BF16), view(16384, [8, 1024], BF16)]
        psm = PS[7]
        wmi = 0
        for l in range(DEPTH):
            for m in range(6):
                wm = wm_bufs[wmi % 2]
                wmi += 1
                P.dma("pool", wm, wmod_d[l, m].rearrange("p (a b) -> p a b", a=8), "wm%d" % (wmi % 2))
                for oc in range(8):
                    o = (m * 8 + oc) * 3
                    for kc in range(8):
                        mm(psm[:, o:o + 3], wm[:, kc, oc * 128:(oc + 1) * 128], SCV[:, kc * 3:kc * 3 + 3],
                           kc == 0, kc == 7)
            tt(MOD[:, l * 144:(l + 1) * 144], psm[:, 0:144], plc(l, PL_BMOD, 144), ALU.add)
            for which, (m_sc, goff) in enumerate(((1, PL_N1G), (4, PL_N2G))):
                for j in range(3):
                    o = (l * 2 + which) * 24
                    src = MOD[:, l * 144 + m_sc * 24:l * 144 + (m_sc + 1) * 24].rearrange("p (k j) -> p k j", j=3)[:, :, j]
                    dst = GS[:, o:o + 24].rearrange("p (k j) -> p k j", j=3)[:, :, j]
                    ts(dst, src, 1.0, None, ALU.add)
                    tt(dst, dst, plc(l, goff, 8), ALU.mult)
            sc0 = MISC[:, 4:5]
            sc1 = MISC[:, 5:6]
            sq64 = MISC[:, 8:16]
            g2 = view(40000, [1, 128], F32)[:, 0, :]
            tt(g2, plc(l, PL_QROW, 128), plc(l, PL_QROW, 128), ALU.mult)
            P.op("dve", lambda e, g2=g2, sc0=sc0: e.tensor_reduce(sc0, g2[:, 0:64], mybir.AxisListType.X, ALU.max),
                 [g2[:, 0:64]], [sc0])
            P.op("dve", lambda e, g2=g2, sc1=sc1: e.tensor_reduce(sc1, g2[:, 64:128], mybir.AxisListType.X, ALU.max),
                 [g2[:, 64:128]], [sc1])
            tt(sc0, sc0, sc1, ALU.add)
            ts(MISC[:, 1 + l:2 + l], sc0, -4.0, None, ALU.mult)

        def rms_rstd(t0, n, sqb, rstd, lntmp, psb):
            for kc in range(8):
                sq = sqb[:, kc % 2, 0:n]
                act(sq, XT[:, kc, t0:t0 + n], AF.Square)
                mm(psb[:, 0:n], O1024, sq, kc == 0, kc == 7)
            rsqrt_act(rstd[:, 0:n], psb[:, 0:n], lntmp[:, 0:n])

        def modulated(t0, n, rstd, tmpA, l, which, j, dst_fn):
            m_sh = 0 if which == 0 else 3
            for kc in range(8):
                tmp = tmpA[:, kc % 2, 0:n]
                tt(tmp, XT[:, kc, t0:t0 + n], rstd[:, 0:n], ALU.mult)
                ts(dst_fn(kc), tmp, gsc(l, which, kc, j), modc(l, m_sh, kc, j), ALU.mult, ALU.add)

        def tap(name, src_ap_fn):
            if name in taps:
                P.dma("sp", taps[name].rearrange("p (a b) -> p a b", a=8), src_ap_fn(), "tap")

        try:
          stage('prologue')
          for s in range(NB):
            stg = [view(50176, [1, 1024], F32)[:, 0, :], view(54272, [1, 1024], F32)[:, 0, :]]
            for ti in range(18):
                st_ = stg[ti % 2]
                src = x_d[s, ti * 128:(ti + 1) * 128, :] if ti < 16 else ctx_d[s, (ti - 16) * 128:(ti - 15) * 128, :]
                P.dma("sp", st_, src, "stg%d" % (ti % 2))
                for half in range(2):
                    pb = PS[(ti * 2 + half) % 4]
                    for q in range(4):
                        kc = half * 4 + q
                        tr(pb[:, q * 128:(q + 1) * 128], st_[:, kc * 128:(kc + 1) * 128], identF)
                    dst = XT[:, half * 4:half * 4 + 4, ti * 128:(ti + 1) * 128]
                    srcp = pb[:, 0:512].rearrange("p (a b) -> p a b", a=4)
                    if half == 0:
                        cp(dst, srcp, "dve")
                    else:
                        cp(dst, srcp, "act")
            if s == 0:
                tap("x0", lambda: XT[:])
            stage('A%d' % s)

            for l in range(DEPTH):
                last = l == DEPTH - 1
                win = view(0, [8, WIN_COLS], BF16)
                diag = view(0, [4, 31, 128], BF16)
                wout = view(0, [8, 1024], BF16)
                ropeC = view(31744, [1, 512], F32)[:, 0, :]
                ropeS = view(35840, [1, 512], F32)[:, 0, :]
                hT = view(39936, [8, 512], BF16)
                convT = view(31744, [4, T], BF16)
                TB = 50176
                sqb = view(TB, [2, 512], BF16)
                sqq2 = view(TB + 2048, [2, 512], BF16)
                rstd = view(TB + 4096, [1, 512], F32)[:, 0, :]
                tmpA = view(TB + 6144, [2, 512], F32)
                qg2 = view(TB + 10240, [2, 512], BF16)
                rq = view(TB + 12288, [1, 512], F32)[:, 0, :]
                t1 = view(TB + 14336, [1, 512], F32)[:, 0, :]
                t2 = view(TB + 16384, [1, 512], F32)[:, 0, :]
                qT = view(68608, [4, T], BF16)
                kT = view(87040, [2, T], BF16)
                Vx = view(96256, [18, 2, 128], BF16)
                gluT = view(105472, [4, GLU_W], BF16)

                P.dma("pool", win, win_d[l].rearrange("p (a b) -> p a b", a=8), "win")
                for kv in range(2):
                    memset(Vx[:, :, kv, 64:128], 1.0)
                memset(gluT[:, :, 0:15], 0.0)
                memset(gluT[:, :, 15 + S:15 + S + 30], 0.0)
                memset(gluT[:, :, GLU_W - 15:GLU_W], 0.0)

                stage('Ba')
                for bi, (t0, n) in enumerate(BLOCKS):
                    isctx = bi == 4
                    j = 2 if isctx else s
                    rms_rstd(t0, n, sqb, rstd, t1, PS[0])
                    modulated(t0, n, rstd, tmpA, l, 0, j, lambda kc: hT[:, kc, 0:n])
                    stage('Bb')
                    if not isctx:
                        P.dma("sp", ropeC[:, 0:n], rope_d[0, :, t0:t0 + n], "rope")
                        P.dma("sp", ropeS[:, 0:n], rope_d[1, :, t0:t0 + n], "rope")
                    do_q = not (isctx and last)
                    do_conv = do_q
                    chunks = ([("q", i) for i in range(4)] if do_q else []) + [("k", 0), ("k", 1)]
                    def qk_stage1(idx, kind, ci):
                        oc = ci if kind == "q" else 4 + ci
                        pa = PS[1 + psi[0] % 3]
                        psi[0] += 1
                        for kc in range(8):
                            mm(pa[:, 0:n], win[:, kc, oc * 128:(oc + 1) * 128], hT[:, kc, 0:n], kc == 0, kc == 7)
                        gcol = plc(l, PL_QG if kind == "q" else PL_KG)
                        act(sqq2[:, idx % 2, 0:n], pa[:, 0:n], AF.Square)
                        ts(qg2[:, idx % 2, 0:n], pa[:, 0:n], gcol, None, ALU.mult)

                    def qk_stage2(idx, kind, ci):
                        sqq = sqq2[:, idx % 2, :]
                        qg = qg2[:, idx % 2, :]
                        mm(PS[4][:, 0:n], B64, sqq[:, 0:n], True, True)
                        if not isctx:
                            mm(PS[5][:, 0:n], Rm, qg[:, 0:n], True, True)
                        rsqrt_act(rq[:, 0:n], PS[4][:, 0:n], t2[:, 0:n])
                        dst = (qT if kind == "q" else kT)[:, ci, t0:t0 + n]
                        if not isctx:
                            tt(t1[:, 0:n], qg[:, 0:n], ropeC[:, 0:n], ALU.mult)
                            tt(t2[:, 0:n], PS[5][:, 0:n], ropeS[:, 0:n], ALU.mult)
                            tt(t1[:, 0:n], t1[:, 0:n], t2[:, 0:n], ALU.add)
                            tt(dst, t1[:, 0:n], rq[:, 0:n], ALU.mult)
                        else:
                            tt(dst, qg[:, 0:n], rq[:, 0:n], ALU.mult)

                    for idx, (kind, ci) in enumerate(chunks):
                        qk_stage1(idx, kind, ci)
                        if idx >= 1:
                            qk_stage2(idx - 1, *chunks[idx - 1])
                    qk_stage2(len(chunks) - 1, *chunks[-1])
                    stage('Bc')
                    if do_conv:
                        goff = 15 + t0 if not isctx else 15 + S + 30
                        for ci in range(4):
                            pa = PS[1 + (2 * ci) % 5]
                            pg = PS[1 + (2 * ci + 1) % 5]
                            for kc in range(8):
                                mm(pa[:, 0:n], win[:, kc, (6 + ci) * 128:(7 + ci) * 128], hT[:, kc, 0:n], kc == 0, kc == 7)
                            for kc in range(8):
                                mm(pg[:, 0:n], win[:, kc, (10 + ci) * 128:(11 + ci) * 128], hT[:, kc, 0:n], kc == 0, kc == 7)
                            act(t2[:, 0:n], pg[:, 0:n], AF.Sigmoid)
                            tt(gluT[:, ci, goff:goff + n], pa[:, 0:n], t2[:, 0:n], ALU.mult)
                    stage('Bd')
                    for tt_ in range(n // 128):
                        pv = PS[6 + tt_ % 2]
                        for kc in range(8):
                            mm(pv[:, 0:128], hT[:, kc, tt_ * 128:(tt_ + 1) * 128], win[:, kc, 1792:1920], kc == 0, kc == 7)
                        kt = (t0 + tt_ * 128) // 128
                        cp(Vx[:, kt, :, 0:64], pv[:, 0:128].rearrange("p (a b) -> p a b", a=2), "act")

                stage('B%d%d' % (s, l))
                for ci in range(4):
                    for k in range(31):
                        ts(diag[:, ci, k, :], identB, plc(l, PL_CDW + ci * 31 + k), None, ALU.mult)
                ysq = view(TB, [4, 512], BF16)
                mean2 = view(TB + 4096, [1, 512], F32)[:, 0, :]
                var = view(TB + 6144, [1, 512], F32)[:, 0, :]
                rstdc = view(TB + 8192, [1, 512], F32)[:, 0, :]
                tcv = view(TB + 10240, [2, 512], F32)
                dblocks = BLOCKS if not last else BLOCKS[:4]
                for bi, (t0, n) in enumerate(dblocks):
                    goff = t0 if bi < 4 else S + 30
                    for ci in range(4):
                        pc = PS[(0, 1, 4, 5)[psi[0] % 4]]
                        psi[0] += 1
                        for k in range(31):
                            mm(pc[:, 0:n], diag[:, ci, k, :], gluT[:, ci, goff + k:goff + k + n], k == 0, k == 30)
                        bcol = plc(l, PL_CDWB + ci)
                        act(convT[:, ci, t0:t0 + n], pc[:, 0:n], AF.Identity, bias=bcol)
                        act(ysq[:, ci, 0:n], pc[:, 0:n], AF.Square, bias=bcol)
                    for ci in range(4):
                        mm(PS[2][:, 0:n], O512, convT[:, ci, t0:t0 + n], ci == 0, ci == 3)
                    for ci in range(4):
                        mm(PS[3][:, 0:n], O512, ysq[:, ci, 0:n], ci == 0, ci == 3)
                    act(mean2[:, 0:n], PS[2][:, 0:n], AF.Square)
                    tt(var[:, 0:n], PS[3][:, 0:n], mean2[:, 0:n], ALU.subtract)
                    rsqrt_act(rstdc[:, 0:n], var[:, 0:n], mean2[:, 0:n])
                    for ci in range(4):
                        tc_ = tcv[:, ci % 2, 0:n]
                        tt(tc_, convT[:, ci, t0:t0 + n], PS[2][:, 0:n], ALU.subtract)
                        tt(tc_, tc_, rstdc[:, 0:n], ALU.mult)
                        act(convT[:, ci, t0:t0 + n], tc_, AF.Silu, bias=plc(l, PL_LNB + ci), scale=plc(l, PL_LNG + ci))

                stage('D%d%d' % (s, l))
                P.dma("pool", wout, wout_d[l].rearrange("p (a b) -> p a b", a=8), "wout")
                PT2 = view(TB, [3, 2, 512], BF16)
                rec = view(TB + 6144, [1, 512], F32)[:, 0, :]
                attnB = view(TB + 8192, [2, 4, 512], BF16)
                negB = MISC[:, 1 + l:2 + l]
                ablocks = BLOCKS if not last else BLOCKS[:4]
                pti = 0
                pending = []
                for bi, (t0, n) in enumerate(ablocks):
                    isctx = bi == 4
                    j = 2 if isctx else s
                    kts = list(range(18)) if not isctx else [16, 17]
                    pairs = [(kts[i], kts[i + 1]) for i in range(0, len(kts), 2)]
                    aT = attnB[:, bi % 2]
                    for h in range(8):
                        use_filler = not pending
                        hp = (h % 2) * 64
                        kv = h // 4
                        po = PS[4 + h % 2]
                        qv = qT[hp:hp + 64, h // 2, t0:t0 + n]

                        def issue_s(pi):
                            pd = PD[pi % 2]
                            for half, kt in enumerate(pairs[pi]):
                                mm(pd[:, half * 512:half * 512 + n], kT[hp:hp + 64, kv, kt * 128:(kt + 1) * 128], qv, True, True)
                            return pd

                        npair = len(pairs)
                        sq_ = [issue_s(0)]
                        if npair > 1:
                            sq_.append(issue_s(1))
                        for pi, (ka, kb) in enumerate(pairs):
                            pd = sq_[pi]
                            pt = PT2[:, pti % 3]
                            pti += 1
                            act(pt[:, :, 0:n], pd[:, :].rearrange("p (a b) -> p a b", a=2)[:, :, 0:n], AF.Exp,
                                bias=negB, scale=0.125)
                            if pi + 2 < npair:
                                sq_.append(issue_s(pi + 2))
                            for half, kt in enumerate((ka, kb)):
                                mm(po[:, 0:n], Vx[:, kt, kv, :], pt[:, half, 0:n], pi == 0 and half == 0,
                                   pi == npair - 1 and half == 1)
                            if use_filler:
                                mm(PS[7][:, 0:n], Vx[:, ka, kv, :], pt[:, 0, 0:n], True, True)
                            elif pi == min(3, npair - 1) and pending:
                                pending.pop(0)()
                        recip(rec[0:64, 0:n], po[64:128, 0:n])
                        tt(aT[hp:hp + 64, h // 2, 0:n], po[0:64, 0:n], rec[0:64, 0:n], ALU.mult)
                    def wout_item(oc, aT=aT, t0=t0, n=n, j=j):
                        pw = PS[6 + oc % 2]
                        for kc in range(8):
                            rhs = aT[:, kc, 0:n] if kc < 4 else convT[:, kc - 4, t0:t0 + n]
                            mm(pw[:, 0:n], wout[:, kc, oc * 128:(oc + 1) * 128], rhs, kc == 0, kc == 7)
                        stt(XT[:, oc, t0:t0 + n], pw[:, 0:n], modc(l, 2, oc, j), XT[:, oc, t0:t0 + n], ALU.mult, ALU.add)
                    while pending:
                        pending.pop(0)()
                    for oc in range(8):
                        pending.append(lambda oc=oc, f=wout_item: f(oc))
                while pending:
                    pending.pop(0)()
                if s == 0:
                    tap("mix%d" % l, lambda: XT[:])
                stage('C%d%d' % (s, l))

                h2T = view(0, [8, T], BF16)
                mbuf = view(36864, [8, T], BF16)
                wdn = view(73728, [8, 1024], BF16)
                wupb = [view(90112 + i * 4096, [8, 256], BF16) for i in range(3)]
                gbuf = view(102400, [1, GB_W], BF16)[:, 0, :]
                dg3 = view(107024, [2, 3, 128], BF16)
                FT = 108560
                sqb2 = view(FT, [4, 512], BF16)
                rstd2 = view(FT + 4096, [1, 512], F32)[:, 0, :]
                tmpA2 = view(FT + 6144, [2, 512], F32)
                asil = view(FT + 10240, [2, 512], F32)
                lnt = asil[:, 0, :]
                fblocks = BLOCKS if not last else BLOCKS[:4]
                for bi, (t0, n) in enumerate(fblocks):
                    j = 2 if bi == 4 else s
                    rms_rstd(t0, n, sqb2, rstd2, lnt, PS[0])
                    modulated(t0, n, rstd2, tmpA2, l, 1, j, lambda kc: h2T[:, kc, t0:t0 + n])
                memset(gbuf[:, 0:1], 0.0)
                memset(gbuf[:, 1 + S:1 + S + 2], 0.0)
                memset(gbuf[:, GB_W - 1:GB_W], 0.0)
                groups = [list(range(0, 8)), list(range(8, 15)), list(range(15, 22))]
                gi, vi = [0], [0]
                wi = 0
                for grp in groups:
                    for jj, jh in enumerate(grp):
                        wu = wupb[wi % 3]
                        P.dma("pool", wu, wup_d[l, jh].rearrange("p (a b) -> p a b", a=8), "wup%d" % (wi % 3))
                        P.dma("pool", wdn[:, jj, :], wdn_d[l, jh], "wdn")
                        dg = dg3[:, wi % 2]
                        wi += 1
                        for bi, (t0, n) in enumerate(fblocks):
                            pgt = PS[gi[0] % 3]
                            gi[0] += 1
                            for kc in range(8):
                                mm(pgt[:, 0:n], wu[:, kc, 0:128], h2T[:, kc, t0:t0 + n], kc == 0, kc == 7)
                            go = 1 + t0 if bi < 4 else 1 + S + 2
                            cp(gbuf[:, go:go + n], pgt[:, 0:n], "act")
                        for bi, (t0, n) in enumerate(fblocks):
                            go = t0 if bi < 4 else S + 2
                            cv = tmpA2[:, bi % 2, 0:n]
                            wk = [plc(l, PL_FDW + jh * 3 + k) for k in range(3)]
                            ts(cv, gbuf[:, go + 1:go + 1 + n], wk[1], plc(l, PL_FDWB + jh), ALU.mult, ALU.add)
                            stt(cv, gbuf[:, go:go + n], wk[0], cv, ALU.mult, ALU.add)
                            stt(cv, gbuf[:, go + 2:go + 2 + n], wk[2], cv, ALU.mult, ALU.add)
                            pvl = PS[3 + vi[0] % 3]
                            vi[0] += 1
                            for kc in range(8):
                                mm(pvl[:, 0:n], wu[:, kc, 128:256], h2T[:, kc, t0:t0 + n], kc == 0, kc == 7)
                            a_ = asil[:, bi % 2, 0:n]
                            act(a_, cv, AF.Silu)
                            tt(mbuf[:, jj, t0:t0 + n], a_, pvl[:, 0:n], ALU.mult)
                    for bi, (t0, n) in enumerate(fblocks):
                        j = 2 if bi == 4 else s
                        for oc in range(8):
                            pd = PS[6 + oc % 2]
                            for jj in range(len(grp)):
                                mm(pd[:, 0:n], wdn[:, jj, oc * 128:(oc + 1) * 128], mbuf[:, jj, t0:t0 + n],
                                   jj == 0, jj == len(grp) - 1)
                            stt(XT[:, oc, t0:t0 + n], pd[:, 0:n], modc(l, 5, oc, j), XT[:, oc, t0:t0 + n],
                                ALU.mult, ALU.add)
                if s == 0:
                    tap("ffn%d" % l, lambda: XT[:])
                stage('F%d%d' % (s, l))

            ofm = view(0, [8, 512], F32)
            ostg = [view(16384, [1, 1024], F32)[:, 0, :], view(20480, [1, 1024], F32)[:, 0, :]]
            sqb3 = view(24576, [4, 512], BF16)
            rstd3 = view(28672, [1, 512], F32)[:, 0, :]
            tmp3 = view(30720, [2, 512], F32)
            lnt3 = view(34816, [1, 512], F32)[:, 0, :]
            oi = 0
            for bi, (t0, n) in enumerate(BLOCKS[:4]):
                rms_rstd(t0, n, sqb3, rstd3, lnt3, PS[0])
                for kc in range(8):
                    tmp = tmp3[:, kc % 2, 0:n]
                    tt(tmp, XT[:, kc, t0:t0 + n], rstd3[:, 0:n], ALU.mult)
                    ts(ofm[:, kc, 0:n], tmp, finalg[:, kc:kc + 1], None, ALU.mult)
                for tq in range(4):
                    og = ostg[oi % 2]
                    for half in range(2):
                        pb = PS[1 + (oi * 2 + half) % 4]
                        for q in range(4):
                            kc = half * 4 + q
                            tr(pb[:, q * 128:(q + 1) * 128], ofm[:, kc, tq * 128:(tq + 1) * 128], identF)
                        cp(og[:, half * 512:(half + 1) * 512], pb[:, 0:512], "act" if half else "dve")
                    P.dma("sp", out_d[s, t0 + tq * 128:t0 + (tq + 1) * 128, :], og, "ost%d" % (oi % 2))
                    oi += 1

        except _Stop:
            pass
        fw = [g for g in ("ost0", "ost1", "tap") if ("dma:" + g) in P.stream_instrs]
        P.emit(final_wait_streams=fw)
        print("program: %d instrs, %d streams" % (P.n_instr, len(P.stream_instrs)),
              {k: len(v) for k, v in P.per_eng.items()})
    return nc


def _fm(v, nchunk):
    return np.ascontiguousarray(np.asarray(v, np.float32).reshape(nchunk, 128).T)


def _host_consts():
    c = np.zeros((128, 5 * 128), np.float32)
    c[:, C_ID:C_ID + 128] = np.eye(128, dtype=np.float32)
    R = np.zeros((128, 128), np.float32)
    for m in range(128):
        if (m % 32) < 16:
            R[m + 16, m] = -1.0
        else:
            R[m - 16, m] = 1.0
    c[:, C_R:C_R + 128] = R
    b = np.zeros((128, 128), np.float32)
    b[0:64, 0:64] = 1.0 / 64
    b[64:128, 64:128] = 1.0 / 64
    c[:, C_B64:C_B64 + 128] = b
    c[:, C_O1024:C_O1024 + 128] = 1.0 / 1024
    c[:, C_O512:C_O512 + 128] = 1.0 / 512
    t = np.arange(S)
    pos = np.stack([t // 64, t % 64], -1).astype(np.float32)
    inv_freq = (np.float32(10000.0) ** (-np.arange(16, dtype=np.float32) / np.float32(16))).astype(np.float32)
    ang = pos[:, :, None] * inv_freq
    cos, sin = np.cos(ang).astype(np.float32), np.sin(ang).astype(np.float32)
    rope = np.zeros((2, 128, S), np.float32)
    for p in range(128):
        d = p % 64
        rope[0, p] = cos[:, d // 32, d % 16]
        rope[1, p] = sin[:, d // 32, d % 16]
    return c, rope


def _prep_shared(inp):
    f = lambda k: np.asarray(inp[k], np.float32)
    consts, rope = _host_consts()
    pl = np.zeros((128, DEPTH * PL_W + 8), np.float32)
    for l in range(DEPTH):
        o = l * PL_W
        pl[:, o + PL_N1G:o + PL_N1G + 8] = _fm(f("norm1_g")[l], 8)
        pl[:, o + PL_N2G:o + PL_N2G + 8] = _fm(f("norm2_g")[l], 8)
        pl[:, o + PL_QG] = np.tile(f("q_norm_g")[l], 2)
        pl[:, o + PL_KG] = np.tile(f("k_norm_g")[l], 2)
        cdw = f("conv_dw")[l]
        pl[:, o + PL_CDW:o + PL_CDW + 124] = cdw.reshape(31, 4, 128).transpose(2, 1, 0).reshape(128, 124)
        pl[:, o + PL_CDWB:o + PL_CDWB + 4] = _fm(f("conv_dw_b")[l], 4)
        pl[:, o + PL_LNG:o + PL_LNG + 4] = _fm(f("conv_ln_g")[l], 4)
        pl[:, o + PL_LNB:o + PL_LNB + 4] = _fm(f("conv_ln_b")[l], 4)
        fdw = f("ffn_dw")[l]
        pl[:, o + PL_FDW:o + PL_FDW + 66] = fdw.reshape(3, NJ, 128).transpose(2, 1, 0).reshape(128, 66)
        pl[:, o + PL_FDWB:o + PL_FDWB + NJ] = _fm(f("ffn_dw_b")[l], NJ)
        bm = _fm(f("b_mod")[l], 48)
        pl[:, o + PL_BMOD:o + PL_BMOD + 144] = np.repeat(bm[:, :, None], 3, axis=2).reshape(128, 144)
        pl[:, o + PL_QROW:o + PL_QROW + 64] = f("q_norm_g")[l][None, :]
        pl[:, o + PL_KROW:o + PL_KROW + 64] = f("k_norm_g")[l][None, :]
    pl[:, DEPTH * PL_W:DEPTH * PL_W + 8] = _fm(f("final_g"), 8)

    def kmaj(w):
        n = w.shape[1]
        return np.ascontiguousarray(w.reshape(8, 128, n).transpose(1, 0, 2)).reshape(128, 8 * n)

    wmod = np.stack([np.stack([kmaj(f("w_mod")[l][:, m * 1024:(m + 1) * 1024]) for m in range(6)]) for l in range(DEPTH)])
    wins = []
    for l in range(DEPTH):
        w = f("w_in")[l]
        q = w[:, 0:512]
        k0, k1 = w[:, 512:576], w[:, 576:640]
        v = w[:, 640:768]
        a = w[:, 768:1280]
        g = w[:, 1280:1792]
        wr = np.concatenate([q, k0, k0, k1, k1, a, g, v], axis=1)
        assert wr.shape[1] == WIN_COLS
        wins.append(kmaj(wr))
    win = np.stack(wins)
    wout = np.stack([kmaj(f("w_out")[l]) for l in range(DEPTH)])
    wups = []
    for l in range(DEPTH):
        w = f("ffn_w_up")[l]
        per = []
        for jh in range(NJ):
            wj = np.concatenate([w[:, jh * 128:(jh + 1) * 128], w[:, HID + jh * 128:HID + (jh + 1) * 128]], axis=1)
            per.append(kmaj(wj))
        wups.append(np.stack(per))
    wup = np.stack(wups)
    wdn = np.stack([f("ffn_w_down")[l].reshape(NJ, 128, 1024) for l in range(DEPTH)])
    return dict(consts=consts, rope=rope, pl=pl, wmod=np.ascontiguousarray(wmod), win=win, wout=wout,
                wup=np.ascontiguousarray(wup), wdn=np.ascontiguousarray(wdn))


def kernel(**inputs):
    n_cores = 8
    run_cores = DBG_CORES or n_cores
    shared = _prep_shared(inputs)
    x = np.asarray(inputs["x"], np.float32)
    c = np.asarray(inputs["c"], np.float32)
    ctx = np.asarray(inputs["ctx"], np.float32)
    c_ctx = np.asarray(inputs["c_ctx"], np.float32)
    in_maps = []
    for i in range(run_cores):
        b0 = i * NB
        vecs = np.stack([c[b0], c[b0 + 1], c_ctx], axis=-1)
        cvec = np.ascontiguousarray(vecs.reshape(8, 128, 3).transpose(1, 0, 2)).reshape(128, 24)
        m = dict(shared)
        m["x"] = np.ascontiguousarray(x[b0:b0 + NB])
        m["ctx"] = np.ascontiguousarray(ctx[b0:b0 + NB])
        m["cvec"] = cvec
        in_maps.append(m)
    nc = build_program()
    res = run_bass_kernel_spmd(nc, in_maps, core_ids=list(range(run_cores)))
    kernel.last_results = res
    out = np.concatenate([np.asarray(r["out"]) for r in res.results], axis=0)
    return out.astype(np.float32)
```
